# Optimizing a Trainium2 kernel written in Bass

```python
import jax, jax.numpy as jnp
from jax import lax
import numpy as np

D_MODEL = 4096
BATCH = 2
SEQ = 4096
DEPTH = 2

HEAD_DIM = 128
BLOCK = 128
A_HEADS = 10
A_KV_HEADS = 2
A_WINDOW = 128
B_HEADS = 10
B_KV_HEADS = 1
IDX_HEADS = 16
IDX_DIM = 64
B_TOPK_MAX = 256
C_GROUPS = ((128, 1), (512, 4), (2048, 16))
C_N_GROUPS = 3
C_HEADS_PER_GROUP = 4
C_HEADS = C_N_GROUPS * C_HEADS_PER_GROUP
D_FF = 11008
CONV_WIDTH = 3
ROPE_THETA = 10000.0
EPS = 1e-6

SPLITS = (
    A_HEADS * HEAD_DIM, A_KV_HEADS * HEAD_DIM, A_KV_HEADS * HEAD_DIM,
    B_HEADS * HEAD_DIM, B_KV_HEADS * HEAD_DIM, B_KV_HEADS * HEAD_DIM,
    IDX_HEADS * IDX_DIM, IDX_DIM, IDX_HEADS,
    C_HEADS * HEAD_DIM, C_N_GROUPS * HEAD_DIM, C_N_GROUPS * HEAD_DIM,
    D_MODEL, D_MODEL, D_MODEL,
)
N_IN = sum(SPLITS)

kernel_name = "hybrid_gated_swa_dsa_dilated_convffn"


def _split_points():
    return [int(v) for v in np.cumsum(SPLITS)[:-1]]


def rmsnorm(x, g):
    xf = x.astype(jnp.float32)
    y = xf * lax.rsqrt(jnp.mean(xf * xf, axis=-1, keepdims=True) + EPS)
    return (y * g.astype(jnp.float32)).astype(x.dtype)


def rope_tables(T, dim):
    pos = jnp.arange(T, dtype=jnp.float32)
    inv = ROPE_THETA ** (-jnp.arange(0, dim, 2, dtype=jnp.float32) / dim)
    ang = pos[:, None] * inv[None, :]
    return jnp.cos(ang), jnp.sin(ang)


def apply_rope(x, cos, sin):
    xf = x.astype(jnp.float32)
    x1, x2 = jnp.split(xf, 2, axis=-1)
    c, s = cos[None, :, None, :], sin[None, :, None, :]
    return jnp.concatenate([x1 * c - x2 * s, x2 * c + x1 * s], axis=-1).astype(x.dtype)


def sliding_window_sink_attn(q, k, v, sinks):
    B_, T, H, D = q.shape
    KV = k.shape[2]
    G = H // KV
    nb = T // BLOCK
    qb = q.reshape(B_, nb, BLOCK, KV, G, D)

    def with_prev(z):
        zb = z.reshape(B_, nb, BLOCK, KV, D)
        prev = jnp.pad(zb, ((0, 0), (1, 0), (0, 0), (0, 0), (0, 0)))[:, :-1]
        return jnp.concatenate([prev, zb], axis=2)

    kb, vb = with_prev(k), with_prev(v)
    s = jnp.einsum('bnqkgd,bnskd->bnkgqs', qb, kb).astype(jnp.float32) * (D ** -0.5)
    qloc = jnp.arange(BLOCK)[:, None] + BLOCK
    kloc = jnp.arange(2 * BLOCK)[None, :]
    dist = qloc - kloc
    band = (dist >= 0) & (dist < A_WINDOW)
    kabs = jnp.arange(nb)[:, None] * BLOCK - BLOCK + jnp.arange(2 * BLOCK)[None, :]
    mask = band[None] & (kabs >= 0)[:, None, :]
    s = jnp.where(mask[None, :, None, None], s, -jnp.inf)
    sink = sinks.astype(jnp.float32).reshape(KV, G)[None, None, :, :, None, None]
    m = jnp.maximum(jnp.max(s, axis=-1, keepdims=True), sink)
    p = jnp.exp(s - m)
    denom = jnp.sum(p, axis=-1, keepdims=True) + jnp.exp(sink - m)
    o = jnp.einsum('bnkgqs,bnskd->bnqkgd', (p / denom).astype(v.dtype), vb)
    return o.reshape(B_, T, H * D)


def dsa_sparse_attn(q, k, v, q_idx, k_idx, w_idx, topk):
    B_, T, H, D = q.shape
    nb = T // BLOCK
    kpos = jnp.arange(T)

    def blk(z):
        return jnp.moveaxis(z.reshape(B_, nb, BLOCK, *z.shape[2:]), 1, 0)

    def one_block(args):
        n, qb, qib, wb = args
        qpos = n * BLOCK + jnp.arange(BLOCK)
        logits = jnp.einsum('bqhe,bse->bqhs', qib, k_idx).astype(jnp.float32)
        score = jnp.einsum('bqh,bqhs->bqs', wb.astype(jnp.float32), jax.nn.relu(logits))
        causal = kpos[None, :] <= qpos[:, None]
        score = jnp.where(causal[None], score, -jnp.inf)
        _, sel = lax.top_k(score, topk)
        valid = sel <= qpos[None, :, None]
        ksel = jax.vmap(lambda kk, ii: kk[ii])(k, sel)
        vsel = jax.vmap(lambda vv, ii: vv[ii])(v, sel)
        s = jnp.einsum('bqhd,bqkd->bhqk', qb, ksel).astype(jnp.float32) * (D ** -0.5)
        s = jnp.where(valid[:, None], s, -jnp.inf)
        p = jax.nn.softmax(s, axis=-1)
        return jnp.einsum('bhqk,bqkd->bqhd', p.astype(v.dtype), vsel)

    out = lax.map(one_block, (jnp.arange(nb), blk(q), blk(q_idx), blk(w_idx)))
    return jnp.moveaxis(out, 0, 1).reshape(B_, T, H * D)


def dilated_attn(q, k, v):
    B_, T, H, D = q.shape
    nb = T // BLOCK
    qg = jnp.moveaxis(q.reshape(B_, nb, BLOCK, C_N_GROUPS, C_HEADS_PER_GROUP, D), 1, 0)
    k_groups = [k[:, :, g] for g in range(C_N_GROUPS)]
    v_groups = [v[:, :, g] for g in range(C_N_GROUPS)]

    def one_block(args):
        n, qb = args
        qpos = n * BLOCK + jnp.arange(BLOCK)
        outs, lses = [], []
        for g, (win, dil) in enumerate(C_GROUPS):
            offs = jnp.arange(win // dil + 1) * dil
            kidx = qpos[:, None] - offs[None, :]
            valid = kidx >= 0
            kidx = jnp.maximum(kidx, 0)
            kg = k_groups[g][:, kidx]
            vg = v_groups[g][:, kidx]
            s = jnp.einsum('bqhd,bqkd->bhqk', qb[:, :, g], kg).astype(jnp.float32) * (D ** -0.5)
            s = jnp.where(valid[None, None], s, -jnp.inf)
            lse = jax.nn.logsumexp(s, axis=-1)
            p = jnp.exp(s - lse[..., None])
            outs.append(jnp.einsum('bhqk,bqkd->bqhd', p.astype(vg.dtype), vg).astype(jnp.float32))
            lses.append(lse)
        alpha = jax.nn.softmax(jnp.stack(lses, axis=0), axis=0)
        alpha = jnp.transpose(alpha, (0, 1, 3, 2))[..., None]
        o = jnp.sum(alpha * jnp.stack(outs, axis=0), axis=0)
        return o.astype(q.dtype)

    out = lax.map(one_block, (jnp.arange(nb), qg))
    return jnp.moveaxis(out, 0, 1).reshape(B_, T, C_HEADS_PER_GROUP * D)


def conv_ffn(h, w_up, conv_w, conv_b, w_down):
    T = h.shape[1]
    u = h @ w_up
    up = jnp.pad(u, ((0, 0), (CONV_WIDTH - 1, 0), (0, 0)))
    c = conv_b
    for j in range(CONV_WIDTH):
        c = c + up[:, j:j + T] * conv_w[j]
    gate, val = jnp.split(c, 2, axis=-1)
    return (jax.nn.silu(gate) * val) @ w_down


def setup_inputs(seed: int = 0) -> dict:
    key = jax.random.key(seed)
    ks = jax.random.split(key, 14)
    f32 = jnp.float32
    n = lambda k, shape, scale: jax.random.normal(k, shape, f32) * scale
    return {
        "x": n(ks[0], (BATCH, SEQ, D_MODEL), 1.0),
        "norm1_g": 1.0 + n(ks[1], (DEPTH, D_MODEL), 0.02),
        "w_in": n(ks[2], (DEPTH, D_MODEL, N_IN), D_MODEL ** -0.5),
        "qk_gain": 1.0 + n(ks[3], (DEPTH, 6, HEAD_DIM), 0.02),
        "sinks": n(ks[4], (DEPTH, A_HEADS), 0.5),
        "w_br_a": n(ks[5], (DEPTH, A_HEADS * HEAD_DIM, D_MODEL), (A_HEADS * HEAD_DIM) ** -0.5),
        "w_br_b": n(ks[6], (DEPTH, B_HEADS * HEAD_DIM, D_MODEL), (B_HEADS * HEAD_DIM) ** -0.5),
        "w_br_c": n(ks[7], (DEPTH, C_HEADS_PER_GROUP * HEAD_DIM, D_MODEL), (C_HEADS_PER_GROUP * HEAD_DIM) ** -0.5),
        "w_out": n(ks[8], (DEPTH, D_MODEL, D_MODEL), D_MODEL ** -0.5),
        "norm2_g": 1.0 + n(ks[9], (DEPTH, D_MODEL), 0.02),
        "w_up": n(ks[10], (DEPTH, D_MODEL, 2 * D_FF), D_MODEL ** -0.5),
        "conv_w": n(ks[11], (DEPTH, CONV_WIDTH, 2 * D_FF), CONV_WIDTH ** -0.5),
        "conv_b": n(ks[12], (DEPTH, 2 * D_FF), 0.01),
        "w_down": n(ks[13], (DEPTH, D_FF, D_MODEL), D_FF ** -0.5),
    }


def reference(x, norm1_g, w_in, qk_gain, sinks, w_br_a, w_br_b, w_br_c, w_out,
              norm2_g, w_up, conv_w, conv_b, w_down):
    B_, T, _ = x.shape
    topk = min(B_TOPK_MAX, T // 4)
    cos, sin = rope_tables(T, HEAD_DIM)
    cos_i, sin_i = rope_tables(T, IDX_DIM)
    points = _split_points()

    def heads(z, nh):
        return z.reshape(B_, T, nh, -1)

    for l in range(DEPTH):
        h = rmsnorm(x, norm1_g[l])
        proj = h @ w_in[l]
        (qa, ka, va, qb, kb, vb, qi, ki, wi, qc, kc, vc, ga, gb, gc) = jnp.split(proj, points, axis=-1)
        g = qk_gain[l]
        qa = apply_rope(rmsnorm(heads(qa, A_HEADS), g[0]), cos, sin)
        ka = apply_rope(rmsnorm(heads(ka, A_KV_HEADS), g[1]), cos, sin)
        oa = sliding_window_sink_attn(qa, ka, heads(va, A_KV_HEADS), sinks[l])
        qb = apply_rope(rmsnorm(heads(qb, B_HEADS), g[2]), cos, sin)
        kb = apply_rope(rmsnorm(heads(kb, B_KV_HEADS), g[3]), cos, sin)[:, :, 0]
        qi = apply_rope(heads(qi, IDX_HEADS), cos_i, sin_i)
        ki = apply_rope(heads(ki, 1), cos_i, sin_i)[:, :, 0]
        ob = dsa_sparse_attn(qb, kb, vb, qi, ki, wi, topk)
        qc = apply_rope(rmsnorm(heads(qc, C_HEADS), g[4]), cos, sin)
        kc = apply_rope(rmsnorm(heads(kc, C_N_GROUPS), g[5]), cos, sin)
        oc = dilated_attn(qc, kc, heads(vc, C_N_GROUPS))
        y = (jax.nn.sigmoid(ga) * (oa @ w_br_a[l])
             + jax.nn.sigmoid(gb) * (ob @ w_br_b[l])
             + jax.nn.sigmoid(gc) * (oc @ w_br_c[l]))
        x = x + y @ w_out[l]
        x = x + conv_ffn(rmsnorm(x, norm2_g[l]), w_up[l], conv_w[l], conv_b[l], w_down[l])
    return x
```

```python
import contextlib
import numpy as np
import concourse.bass as bass
import concourse.mybir as mybir
from concourse.bass_utils import run_bass_kernel_spmd

F32 = mybir.dt.float32
BF16 = mybir.dt.bfloat16
ALU = mybir.AluOpType
AF = mybir.ActivationFunctionType

NT = 1024
D = 4096
KC = 32
NIN = 19024
DFF = 11008
EPS = 1e-6
NEG = -1.0e30

O_QA, O_KA, O_VA, O_QB, O_KB, O_VB, O_QI, O_KI, O_WI, O_QC, O_KC, O_VC, O_GA = (
    0, 1280, 1536, 1792, 3072, 3200, 3328, 4352, 4416, 4432, 5968, 6352, 6736)
F_QA, F_KA, F_QB, F_KB, F_QI, F_KI, F_QC, F_KCC = 0, 10, 12, 22, 23, 31, 32, 44
NFM = 47


class Buf:
    __slots__ = ("name", "w", "r", "dsem", "dcnt")

    def __init__(self, name):
        self.name = name
        self.w = None
        self.r = {}
        self.dsem = None
        self.dcnt = 0


class Ctx:
    def __init__(self, nc):
        self.nc = nc
        self.engs = {}
        for n, e in [("pe", nc.tensor), ("act", nc.scalar), ("dve", nc.vector),
                     ("pool", nc.gpsimd), ("sp", nc.sync)]:
            self.engs[n] = dict(e=e, sem=nc.alloc_semaphore(name="s_" + n), cnt=0, waited={})
        self.dbufs = []
        self.uid = 0

    def _wait(self, en, tok):
        if tok is None:
            return
        sem, val, src = tok
        if src == en and en == "pe":
            return
        E = self.engs[en]
        key = id(sem)
        if E["waited"].get(key, 0) >= val:
            return
        E["e"].wait_ge(sem, val)
        E["waited"][key] = val

    def _deps(self, en, reads, writes):
        for b in reads:
            self._wait(en, b.w)
        for b in writes:
            self._wait(en, b.w)
            for t in b.r.values():
                self._wait(en, t)

    def op(self, en, fn, reads=(), writes=()):
        self._deps(en, reads, writes)
        E = self.engs[en]
        ins = fn(E["e"])
        E["cnt"] += 1
        ins.then_inc(E["sem"], 1)
        tok = (E["sem"], E["cnt"], en)
        for b in reads:
            b.r[en] = tok
        for b in writes:
            b.w = tok
            b.r = {}
        return tok

    def dma(self, qn, out, in_, sembuf, reads=(), writes=()):
        self._deps(qn, reads, writes)
        E = self.engs[qn]
        if sembuf.dsem is None:
            self.uid += 1
            sembuf.dsem = self.nc.alloc_semaphore(name="d%d_%s" % (self.uid, sembuf.name))
            self.dbufs.append(sembuf)
        ins = E["e"].dma_start(out=out, in_=in_)
        sembuf.dcnt += 16
        ins.then_inc(sembuf.dsem, 16)
        tok = (sembuf.dsem, sembuf.dcnt, "dma")
        for b in reads:
            b.r[("dma", id(sembuf.dsem))] = tok
        for b in writes:
            b.w = tok
            b.r = {}
        return tok

    def barrier(self):
        for en, E in self.engs.items():
            for fn, Fe in self.engs.items():
                if fn != en and Fe["cnt"] > 0:
                    self._wait(en, (Fe["sem"], Fe["cnt"], fn))
            for b in self.dbufs:
                self._wait(en, (b.dsem, b.dcnt, "dma"))


class T:
    def __init__(self, t, name):
        self.t = t
        self.b = Buf(name)


def sb(nc, es, name, shape, dt):
    return T(es.enter_context(nc.sbuf_tensor("s_" + name, shape, dt)), name)


def make_psum(nc, es):
    return [T(es.enter_context(nc.psum_tensor("ps%d" % i, [128, 512], F32)), "ps%d" % i) for i in range(8)]


class WStream:
    def __init__(self, c, nc, es, SW, KCmax, nstg=3, nslots=2, tag="w"):
        self.c, self.nc, self.SW = c, nc, SW
        self.stg = [sb(nc, es, "%s_stg%d" % (tag, i), [128, 8, SW], F32) for i in range(nstg)]
        self.slots = []
        for i in range(nslots):
            t = es.enter_context(nc.sbuf_tensor("%s_bf%d" % (tag, i), [128, KCmax, SW], BF16))
            self.slots.append((t, [Buf("%s_bf%d_%d" % (tag, i, g)) for g in range((KCmax + 7) // 8)]))
        self.sctr = 0
        self.cctr = 0
        self.ceng = ["pool", "dve", "pool", "act"]

    def load(self, slot, wd, nrows, ranges):
        t, bufs = self.slots[slot]
        W = max(sc + n for (_, n, sc) in ranges)
        nk = nrows // 128
        for g in range((nk + 7) // 8):
            k0 = g * 8
            kn = min(8, nk - k0)
            st = self.stg[self.sctr % len(self.stg)]
            self.sctr += 1
            for (dc, n, sc) in ranges:
                src = wd[k0 * 128:(k0 + kn) * 128, dc:dc + n].rearrange("(k p) n -> p k n", p=128)
                self.c.dma("sp", st.t[:, 0:kn, sc:sc + n], src, st.b, writes=[st.b])
            en = self.ceng[self.cctr % len(self.ceng)]
            self.cctr += 1
            o = t[:, k0:k0 + kn, 0:W]
            i = st.t[:, 0:kn, 0:W]
            if en == "act":
                self.c.op("act", lambda e: e.copy(out=o, in_=i), reads=[st.b], writes=[bufs[g]])
            else:
                self.c.op(en, lambda e: e.tensor_copy(out=o, in_=i), reads=[st.b], writes=[bufs[g]])


def load_const(c, nc, es, name, dram_ap, shape, dt=F32, cast=None):
    t = sb(nc, es, name, shape, dt)
    c.dma("sp", t.t[tuple(slice(None) for _ in shape)], dram_ap, t.b, writes=[t.b])
    if cast is None:
        return t
    t2 = sb(nc, es, name + "_c", shape, cast)
    sl = tuple(slice(None) for _ in shape)
    c.op("dve", lambda e: e.tensor_copy(out=t2.t[sl], in_=t.t[sl]), reads=[t.b], writes=[t2.b])
    return t2


def emit_norm_T(c, nc, es_outer, PS, x_rows_fn, ntiles, g_rep, ident, hT, col_of_tile, ncols_of_tile, eps_t):
    with contextlib.ExitStack() as es:
        xt = [sb(nc, es, "n_xt%d" % i, [128, D], F32) for i in range(2)]
        xg = [sb(nc, es, "n_xg%d" % i, [128, D], F32) for i in range(1)]
        junk = sb(nc, es, "n_junk", [128, D], BF16)
        ss = sb(nc, es, "n_ss", [128, 16], F32)
        rs = sb(nc, es, "n_rs", [128, 16], F32)
        ev = 0
        for j in range(ntiles):
            X = xt[j % 2]
            G = xg[0]
            ap, rows = x_rows_fn(j)
            if rows < 128:
                c.op("dve", lambda e: e.memset(X.t[:, :], 0.0), writes=[X.b])
            c.dma("sp", X.t[0:rows, :], ap, X.b, writes=[X.b])
            c.op("act", lambda e: e.activation(out=junk.t[:, :], in_=X.t[:, :], func=AF.Square,
                                               accum_out=ss.t[:, j:j + 1]), reads=[X.b], writes=[junk.b, ss.b])
            c.op("act", lambda e: e.activation(out=rs.t[:, j:j + 1], in_=ss.t[:, j:j + 1], func=AF.Sqrt,
                                               scale=1.0 / D, bias=eps_t.t[:, 0:1]), reads=[ss.b, eps_t.b], writes=[rs.b])
            c.op("dve", lambda e: e.reciprocal(out=rs.t[:, j:j + 1], in_=rs.t[:, j:j + 1]), reads=[rs.b], writes=[rs.b])
            c.op("dve", lambda e: e.scalar_tensor_tensor(out=G.t[:, :], in0=X.t[:, :], scalar=rs.t[:, j:j + 1],
                                                         in1=g_rep.t[:, :], op0=ALU.mult, op1=ALU.mult),
                 reads=[X.b, rs.b, g_rep.b], writes=[G.b])
            nco = ncols_of_tile(j)
            c0 = col_of_tile(j)
            for cg in range(8):
                P = PS[cg % 4]
                for i in range(4):
                    ch = cg * 4 + i
                    c.op("pe", lambda e: e.transpose(out=P.t[:, i * 128:(i + 1) * 128], in_=G.t[:, ch * 128:(ch + 1) * 128],
                                                     identity=ident.t[:, :]), reads=[G.b, ident.b], writes=[P.b])
                src = P.t[:, :].rearrange("p (c t) -> p c t", c=4)[:, :, 0:nco]
                dst = hT.t[:, cg * 4:(cg + 1) * 4, c0:c0 + nco]
                if ev % 2 == 0:
                    c.op("act", lambda e: e.copy(out=dst, in_=src), reads=[P.b], writes=[hT.b])
                else:
                    c.op("dve", lambda e: e.tensor_copy(out=dst, in_=src), reads=[P.b], writes=[hT.b])
                ev += 1
        c.barrier()


def build_L1():
    nc = bass.Bass("TRN2", target_bir_lowering=False)
    c = Ctx(nc)
    x_d = nc.dram_tensor("x", [NT, D], F32, kind="ExternalInput").ap()
    w_d = nc.dram_tensor("w_in", [D, NIN], F32, kind="ExternalInput").ap()
    g1_d = nc.dram_tensor("g1", [128, D], F32, kind="ExternalInput").ap()
    gain_d = nc.dram_tensor("gain", [128, 6], F32, kind="ExternalInput").ap()
    cs_d = nc.dram_tensor("cs", [128, 4, NT], F32, kind="ExternalInput").ap()
    cst_d = nc.dram_tensor("cst", [128, 4, 128], F32, kind="ExternalInput").ap()
    fm_d = nc.dram_tensor("fm", [NFM, 128, NT], BF16, kind="ExternalOutput").ap()
    v_d = nc.dram_tensor("v", [NT, 768], BF16, kind="ExternalOutput").ap()
    wi_d = nc.dram_tensor("wi", [NT, 16], F32, kind="ExternalOutput").ap()
    gt_d = nc.dram_tensor("gt", [96, 128, NT], F32, kind="ExternalOutput").ap()
    out_b = Buf("outs")

    with contextlib.ExitStack() as es0:
        PS = make_psum(nc, es0)
        hT = sb(nc, es0, "hT", [128, KC, NT], BF16)
        cst = load_const(c, nc, es0, "cst", cst_d, [128, 4, 128])
        cstb = sb(nc, es0, "cstb", [128, 4, 128], BF16)
        c.op("dve", lambda e: e.tensor_copy(out=cstb.t[:, :, :], in_=cst.t[:, :, :]), reads=[cst.b], writes=[cstb.b])
        gain = load_const(c, nc, es0, "gain", gain_d, [128, 6])
        eps_t = sb(nc, es0, "eps", [128, 1], F32)
        c.op("dve", lambda e: e.memset(eps_t.t[:, :], EPS), writes=[eps_t.b])
        ident = T(cst.t[:, 0, :], "ident")
        ident.b = cst.b
        with contextlib.ExitStack() as es1:
            g1 = load_const(c, nc, es1, "g1", g1_d, [128, D])
            emit_norm_T(c, nc, es1, PS, lambda j: (x_d[j * 128:(j + 1) * 128, :], 128), 8, g1, ident, hT,
                        lambda j: j * 128, lambda j: 128, eps_t)
        with contextlib.ExitStack() as es:
            cs = load_const(c, nc, es, "cs", cs_d, [128, 4, NT])
            ws = WStream(c, nc, es, 256, KC, nstg=3, nslots=2, tag="w")
            tmp = []
            for i in range(2):
                tmp.append(dict(
                    t0=sb(nc, es, "t0_%d" % i, [128, NT], F32), sq=sb(nc, es, "sq_%d" % i, [128, NT], BF16),
                    rs=sb(nc, es, "rs_%d" % i, [128, NT], F32), xg=sb(nc, es, "xg_%d" % i, [128, NT], BF16),
                    qo=sb(nc, es, "qo_%d" % i, [128, NT], BF16)))
            vt = [sb(nc, es, "vt%d" % i, [128, 256], BF16) for i in range(2)]
            wt = [sb(nc, es, "wt%d" % i, [128, 16], F32) for i in range(2)]
            ones_b = cstb.t[:, 1, :]
            blocks = []

            def fm_blocks(col0, nchunks, fm0, typ, gi):
                k = 0
                while k < nchunks:
                    n = min(2, nchunks - k)
                    blocks.append(dict(mode="fm", ranges=[(col0 + k * 128, n * 128, 0)],
                                       chunks=[(fm0 + k + i, typ, gi) for i in range(n)]))
                    k += n

            fm_blocks(O_QA, 10, F_QA, "rn", 0)
            fm_blocks(O_KA, 2, F_KA, "rn", 1)
            blocks.append(dict(mode="tm", ranges=[(O_VA, 256, 0)], out=(v_d, 0, 256)))
            fm_blocks(O_QB, 10, F_QB, "rn", 2)
            fm_blocks(O_KB, 1, F_KB, "rn", 3)
            blocks.append(dict(mode="tm", ranges=[(O_VB, 128, 0)], out=(v_d, 256, 128)))
            fm_blocks(O_QI, 8, F_QI, "ri", 0)
            blocks.append(dict(mode="fm", ranges=[(O_KI, 64, 0), (O_KI, 64, 64)], chunks=[(F_KI, "ri", 0)]))
            blocks.append(dict(mode="tm", ranges=[(O_WI, 16, 0)], out=(wi_d, 0, 16)))
            fm_blocks(O_QC, 12, F_QC, "rn", 4)
            fm_blocks(O_KC, 3, F_KCC, "rn", 5)
            blocks.append(dict(mode="tm", ranges=[(O_VC, 256, 0)], out=(v_d, 384, 256)))
            blocks.append(dict(mode="tm", ranges=[(O_VC + 256, 128, 0)], out=(v_d, 640, 128)))
            fm_blocks(O_GA, 96, 1000, "sig", 0)

            cctr = [0]
            tctr = [0]

            def do_chunk(slot, ci, fmi, typ, gi):
                wt_, wb = ws.slots[slot]
                par = cctr[0] % 2
                cctr[0] += 1
                pp = [PS[par * 2], PS[par * 2 + 1]]
                for th in range(2):
                    for k in range(KC):
                        c.op("pe", lambda e: e.matmul(pp[th].t[:, :], lhsT=wt_[:, k, ci * 128:(ci + 1) * 128],
                                                      rhs=hT.t[:, k, th * 512:(th + 1) * 512],
                                                      start=(k == 0), stop=(k == KC - 1)),
                             reads=[wb[k // 8], hT.b], writes=[pp[th].b])
                tt = tmp[par]
                t0, sq, rs, xg, qo = tt["t0"], tt["sq"], tt["rs"], tt["xg"], tt["qo"]
                H = [slice(0, 512), slice(512, 1024)]
                if typ == "sig":
                    for th in range(2):
                        c.op("act", lambda e: e.activation(out=t0.t[:, H[th]], in_=pp[th].t[:, :], func=AF.Sigmoid),
                             reads=[pp[th].b], writes=[t0.b])
                    c.dma("sp", gt_d[fmi - 1000], t0.t[:, :], out_b, reads=[t0.b])
                    return
                if typ == "rn":
                    for th in range(2):
                        c.op("act", lambda e: e.copy(out=t0.t[:, H[th]], in_=pp[th].t[:, :]), reads=[pp[th].b], writes=[t0.b])
                    c.op("pool", lambda e: e.tensor_tensor(out=sq.t[:, :], in0=t0.t[:, :], in1=t0.t[:, :], op=ALU.mult),
                         reads=[t0.b], writes=[sq.b])
                    for th in range(2):
                        c.op("pe", lambda e: e.matmul(PS[4 + th].t[:, :], lhsT=ones_b, rhs=sq.t[:, H[th]], start=True, stop=True),
                             reads=[cstb.b, sq.b], writes=[PS[4 + th].b])
                    for th in range(2):
                        c.op("act", lambda e: e.activation(out=rs.t[:, H[th]], in_=PS[4 + th].t[:, :], func=AF.Sqrt,
                                                           scale=1.0 / 128, bias=eps_t.t[:, 0:1]),
                             reads=[PS[4 + th].b, eps_t.b], writes=[rs.b])
                    c.op("dve", lambda e: e.reciprocal(out=rs.t[:, :], in_=rs.t[:, :]), reads=[rs.b], writes=[rs.b])
                    c.op("dve", lambda e: e.scalar_tensor_tensor(out=xg.t[:, :], in0=t0.t[:, :], scalar=gain.t[:, gi:gi + 1],
                                                                 in1=rs.t[:, :], op0=ALU.mult, op1=ALU.mult),
                         reads=[t0.b, gain.b, rs.b], writes=[xg.b])
                    perm = cstb.t[:, 2, :]
                    cosT, sinT = cs.t[:, 0, :], cs.t[:, 1, :]
                else:
                    for th in range(2):
                        c.op("act", lambda e: e.copy(out=xg.t[:, H[th]], in_=pp[th].t[:, :]), reads=[pp[th].b], writes=[xg.b])
                    perm = cstb.t[:, 3, :]
                    cosT, sinT = cs.t[:, 2, :], cs.t[:, 3, :]
                for th in range(2):
                    c.op("pe", lambda e: e.matmul(PS[6 + th].t[:, :], lhsT=perm, rhs=xg.t[:, H[th]], start=True, stop=True),
                         reads=[cstb.b, xg.b], writes=[PS[6 + th].b])
                c.op("dve", lambda e: e.tensor_tensor(out=rs.t[:, :], in0=xg.t[:, :], in1=cosT, op=ALU.mult),
                     reads=[xg.b, cs.b], writes=[rs.b])
                for th in range(2):
                    c.op("dve", lambda e: e.tensor_tensor(out=t0.t[:, H[th]], in0=PS[6 + th].t[:, :], in1=sinT[:, H[th]], op=ALU.mult),
                         reads=[PS[6 + th].b, cs.b], writes=[t0.b])
                c.op("pool", lambda e: e.tensor_tensor(out=qo.t[:, :], in0=rs.t[:, :], in1=t0.t[:, :], op=ALU.add),
                     reads=[rs.b, t0.b], writes=[qo.b])
                c.dma("sp", fm_d[fmi], qo.t[:, :], out_b, reads=[qo.b])

            def do_tm(slot, blk):
                wt_, wb = ws.slots[slot]
                od, oc, W = blk["out"]
                for j in range(8):
                    P = PS[(tctr[0] % 2) * 2]
                    if W == 16:
                        V = wt[tctr[0] % 2]
                    else:
                        V = vt[tctr[0] % 2]
                    tctr[0] += 1
                    for k in range(KC):
                        c.op("pe", lambda e: e.matmul(P.t[:, 0:W], lhsT=hT.t[:, k, j * 128:(j + 1) * 128], rhs=wt_[:, k, 0:W],
                                                      start=(k == 0), stop=(k == KC - 1)),
                             reads=[wb[k // 8], hT.b], writes=[P.b])
                    c.op("act", lambda e: e.copy(out=V.t[:, 0:W], in_=P.t[:, 0:W]), reads=[P.b], writes=[V.b])
                    c.dma("sp", od[j * 128:(j + 1) * 128, oc:oc + W], V.t[:, 0:W], out_b, reads=[V.b])

            ws.load(0, w_d, D, blocks[0]["ranges"])
            for bi, blk in enumerate(blocks):
                slot = bi % 2
                if bi + 1 < len(blocks):
                    ws.load((bi + 1) % 2, w_d, D, blocks[bi + 1]["ranges"])
                if blk["mode"] == "fm":
                    for ci, (fmi, typ, gi) in enumerate(blk["chunks"]):
                        do_chunk(slot, ci, fmi, typ, gi)
                else:
                    do_tm(slot, blk)
            c.barrier()
    return nc


def build_L3():
    nc = bass.Bass("TRN2", target_bir_lowering=False)
    c = Ctx(nc)
    xm_d = nc.dram_tensor("xm", [NT, D], F32, kind="ExternalInput").ap()
    halo_d = nc.dram_tensor("halo", [2, D], F32, kind="ExternalInput").ap()
    g2_d = nc.dram_tensor("g2", [128, D], F32, kind="ExternalInput").ap()
    wup_d = nc.dram_tensor("w_up", [D, 2 * DFF], F32, kind="ExternalInput").ap()
    wdn_d = nc.dram_tensor("w_down", [DFF, D], F32, kind="ExternalInput").ap()
    cw_d = nc.dram_tensor("cw", [128, 172, 4], F32, kind="ExternalInput").ap()
    cst_d = nc.dram_tensor("cst", [128, 4, 128], F32, kind="ExternalInput").ap()
    xo_d = nc.dram_tensor("xo", [NT, D], F32, kind="ExternalOutput").ap()
    sT_d = nc.dram_tensor("sT_scr", [86, 128, NT], BF16, kind="Internal").ap()
    out_b = Buf("outs")
    sT_b = Buf("sT_scr")
    NTH = NT + 2
    with contextlib.ExitStack() as es0:
        PS = make_psum(nc, es0)
        cst = load_const(c, nc, es0, "cst", cst_d, [128, 4, 128])
        eps_t = sb(nc, es0, "eps", [128, 1], F32)
        c.op("dve", lambda e: e.memset(eps_t.t[:, :], EPS), writes=[eps_t.b])
        ident = T(cst.t[:, 0, :], "ident")
        ident.b = cst.b
        with contextlib.ExitStack() as esA:
            hT = sb(nc, esA, "hT", [128, KC, NTH], BF16)
            with contextlib.ExitStack() as es1:
                g2 = load_const(c, nc, es1, "g2", g2_d, [128, D])
                emit_norm_T(c, nc, es1, PS,
                            lambda j: (xm_d[j * 128:(j + 1) * 128, :], 128) if j < 8 else (halo_d, 2),
                            9, g2, ident, hT, lambda j: 2 + j * 128 if j < 8 else 0,
                            lambda j: 128 if j < 8 else 2, eps_t)
            with contextlib.ExitStack() as es:
                cw = load_const(c, nc, es, "cw", cw_d, [128, 172, 4])
                ws = WStream(c, nc, es, 256, KC, nstg=3, nslots=2, tag="wu")
                U = [sb(nc, es, "U%d" % i, [128, NTH], F32) for i in range(2)]
                SG = [sb(nc, es, "SG%d" % i, [128, NT], F32) for i in range(4)]
                CV = [sb(nc, es, "CV%d" % i, [128, NT], F32) for i in range(2)]
                SO = [sb(nc, es, "SO%d" % i, [128, NT], BF16) for i in range(2)]
                cctr = [0]

                def up_chunk(slot, ci, cc_idx):
                    wt_, wb = ws.slots[slot]
                    par = cctr[0] % 2
                    cctr[0] += 1
                    A, B, Hh = PS[par * 2], PS[par * 2 + 1], PS[4 + par]
                    for (P, lo, n) in ((A, 2, 512), (B, 514, 512), (Hh, 0, 2)):
                        for k in range(KC):
                            c.op("pe", lambda e: e.matmul(P.t[:, 0:n], lhsT=wt_[:, k, ci * 128:(ci + 1) * 128],
                                                          rhs=hT.t[:, k, lo:lo + n], start=(k == 0), stop=(k == KC - 1)),
                                 reads=[wb[k // 8], hT.b], writes=[P.b])
                    u = U[par]
                    c.op("act", lambda e: e.copy(out=u.t[:, 2:514], in_=A.t[:, :]), reads=[A.b], writes=[u.b])
                    c.op("act", lambda e: e.copy(out=u.t[:, 514:1026], in_=B.t[:, :]), reads=[B.b], writes=[u.b])
                    c.op("act", lambda e: e.copy(out=u.t[:, 0:2], in_=Hh.t[:, 0:2]), reads=[Hh.b], writes=[u.b])
                    return u

                def conv(u, dst, cc_idx):
                    c.op("dve", lambda e: e.tensor_scalar(out=dst.t[:, :], in0=u.t[:, 0:NT], scalar1=cw.t[:, cc_idx, 0:1],
                                                          scalar2=cw.t[:, cc_idx, 3:4], op0=ALU.mult, op1=ALU.add),
                         reads=[u.b, cw.b], writes=[dst.b])
                    for tap in (1, 2):
                        c.op("dve", lambda e: e.scalar_tensor_tensor(out=dst.t[:, :], in0=u.t[:, tap:tap + NT],
                                                                     scalar=cw.t[:, cc_idx, tap:tap + 1], in1=dst.t[:, :],
                                                                     op0=ALU.mult, op1=ALU.add),
                             reads=[u.b, cw.b, dst.b], writes=[dst.b])

                blks = []
                for pb in range(43):
                    blks.append(("g", pb, [(pb * 256, 256, 0)]))
                    blks.append(("v", pb, [(DFF + pb * 256, 256, 0)]))
                ws.load(0, wup_d, D, blks[0][2])
                for bi, (kind, pb, rng) in enumerate(blks):
                    slot = bi % 2
                    if bi + 1 < len(blks):
                        ws.load((bi + 1) % 2, wup_d, D, blks[bi + 1][2])
                    for ci in range(2):
                        m = pb * 2 + ci
                        sg = SG[(pb % 2) * 2 + ci]
                        if kind == "g":
                            u = up_chunk(slot, ci, m)
                            conv(u, sg, m)
                            c.op("act", lambda e: e.activation(out=sg.t[:, :], in_=sg.t[:, :], func=AF.Silu),
                                 reads=[sg.b], writes=[sg.b])
                        else:
                            u = up_chunk(slot, ci, 86 + m)
                            cv = CV[ci]
                            conv(u, cv, 86 + m)
                            so = SO[ci]
                            c.op("pool", lambda e: e.tensor_tensor(out=so.t[:, :], in0=cv.t[:, :], in1=sg.t[:, :], op=ALU.mult),
                                 reads=[cv.b, sg.b], writes=[so.b])
                            c.dma("sp", sT_d[m], so.t[:, :], sT_b, reads=[so.b], writes=[sT_b])
                c.barrier()
        with contextlib.ExitStack() as es:
            wd = WStream(c, nc, es, 512, 8, nstg=3, nslots=3, tag="wd")
            SK = [sb(nc, es, "SK%d" % i, [128, 8, NT], BF16) for i in range(2)]
            R = [sb(nc, es, "R%d" % i, [128, 512], F32) for i in range(2)]
            O = [sb(nc, es, "O%d" % i, [128, 512], F32) for i in range(2)]
            steps = [(cb, kg) for cb in range(8) for kg in range(11)]

            def ld(si):
                cb, kg = steps[si]
                kn = 8 if kg < 10 else 6
                wd.load(si % 3, wdn_d[kg * 1024:kg * 1024 + kn * 128, :], kn * 128, [(cb * 512, 512, 0)])
                sk = SK[si % 2]
                c.dma("sp", sk.t[:, 0:kn, :], sT_d[kg * 8:kg * 8 + kn].rearrange("k p t -> p k t"), sk.b,
                      reads=[sT_b], writes=[sk.b])

            ld(0)
            ectr = 0
            for si, (cb, kg) in enumerate(steps):
                if si + 1 < len(steps):
                    ld(si + 1)
                kn = 8 if kg < 10 else 6
                wt_, wb = wd.slots[si % 3]
                sk = SK[si % 2]
                for kk in range(kn):
                    k = kg * 8 + kk
                    for j in range(8):
                        c.op("pe", lambda e: e.matmul(PS[j].t[:, :], lhsT=sk.t[:, kk, j * 128:(j + 1) * 128], rhs=wt_[:, kk, 0:512],
                                                      start=(k == 0), stop=(k == 85)),
                             reads=[wb[0], sk.b], writes=[PS[j].b])
                if kg == 10:
                    for j in range(8):
                        r, o = R[ectr % 2], O[ectr % 2]
                        ectr += 1
                        c.dma("sp", r.t[:, :], xm_d[j * 128:(j + 1) * 128, cb * 512:(cb + 1) * 512], r.b, writes=[r.b])
                        c.op("dve", lambda e: e.tensor_tensor(out=o.t[:, :], in0=PS[j].t[:, :], in1=r.t[:, :], op=ALU.add),
                             reads=[PS[j].b, r.b], writes=[o.b])
                        c.dma("sp", xo_d[j * 128:(j + 1) * 128, cb * 512:(cb + 1) * 512], o.t[:, :], out_b, reads=[o.b])
            c.barrier()
    return nc


WB_C = (1, 4, 16)
WOFF_C = (0, 9, 21)
NWC = 45


def build_L2():
    nc = bass.Bass("TRN2", target_bir_lowering=False)
    c = Ctx(nc)
    fm_d = nc.dram_tensor("fm", [NFM, 128, NT], BF16, kind="ExternalInput").ap()
    kwa_d = nc.dram_tensor("kwa", [2, 128, 1152], BF16, kind="ExternalInput").ap()
    vwa_d = nc.dram_tensor("vwa", [1152, 256], BF16, kind="ExternalInput").ap()
    kwc_d = nc.dram_tensor("kwc", [128, NWC * 128], BF16, kind="ExternalInput").ap()
    vwc_d = nc.dram_tensor("vwc", [NWC * 128, 128], BF16, kind="ExternalInput").ap()
    kb_d = nc.dram_tensor("kb", [128, 4096], BF16, kind="ExternalInput").ap()
    ki_d = nc.dram_tensor("ki", [128, 4096], BF16, kind="ExternalInput").ap()
    vb_d = nc.dram_tensor("vb", [4096, 128], BF16, kind="ExternalInput").ap()
    wi_d = nc.dram_tensor("wi", [NT, 16], F32, kind="ExternalInput").ap()
    gt_d = nc.dram_tensor("gt", [96, 128, NT], F32, kind="ExternalInput").ap()
    x_d = nc.dram_tensor("x", [NT, D], F32, kind="ExternalInput").ap()
    sk_d = nc.dram_tensor("sinks", [128, 10], F32, kind="ExternalInput").ap()
    kv_d = nc.dram_tensor("kvalid", [128, 9 + NWC], F32, kind="ExternalInput").ap()
    thr_d = nc.dram_tensor("thr", [128, 8], F32, kind="ExternalInput").ap()
    iota_d = nc.dram_tensor("iota", [128, 4096], F32, kind="ExternalInput").ap()
    mk_d = nc.dram_tensor("masks", [128, 9, 128], F32, kind="ExternalInput").ap()
    cst_d = nc.dram_tensor("cst", [128, 4, 128], F32, kind="ExternalInput").ap()
    wa_d = nc.dram_tensor("w_br_a", [1280, D], F32, kind="ExternalInput").ap()
    wb_d = nc.dram_tensor("w_br_b", [1280, D], F32, kind="ExternalInput").ap()
    wc_d = nc.dram_tensor("w_br_c", [512, D], F32, kind="ExternalInput").ap()
    wo_d = nc.dram_tensor("w_out", [D, D], F32, kind="ExternalInput").ap()
    xm_d = nc.dram_tensor("xm", [NT, D], F32, kind="ExternalOutput").ap()
    out_b = Buf("outs")
    SC = 1.0 / float(np.sqrt(128.0))

    with contextlib.ExitStack() as es0:
        PS = make_psum(nc, es0)
        cst = load_const(c, nc, es0, "cst", cst_d, [128, 4, 128])
        cstb = sb(nc, es0, "cstb", [128, 4, 128], BF16)
        c.op("dve", lambda e: e.tensor_copy(out=cstb.t[:, :, :], in_=cst.t[:, :, :]), reads=[cst.b], writes=[cstb.b])
        ones_b = cstb.t[:, 1, :]
        with contextlib.ExitStack() as esO:
            oTa = sb(nc, esO, "oTa", [128, 10, NT], BF16)
            oTb = sb(nc, esO, "oTb", [128, 10, NT], BF16)
            oTc = sb(nc, esO, "oTc", [128, 4, NT], BF16)
            with contextlib.ExitStack() as es:
                mk32 = load_const(c, nc, es, "mk32", mk_d, [128, 9, 128])
                mk = sb(nc, es, "mk", [128, 9, 128], BF16)
                c.op("dve", lambda e: e.tensor_copy(out=mk.t[:, :, :], in_=mk32.t[:, :, :]), reads=[mk32.b], writes=[mk.b])
                kvd = load_const(c, nc, es, "kvd", kv_d, [128, 9 + NWC])
                thr = load_const(c, nc, es, "thr", thr_d, [128, 8])
                sk = load_const(c, nc, es, "sk", sk_d, [128, 10])
                esk = sb(nc, es, "esk", [128, 10], F32)
                c.op("act", lambda e: e.activation(out=esk.t[:, :], in_=sk.t[:, :], func=AF.Exp), reads=[sk.b], writes=[esk.b])
                wi = sb(nc, es, "wi", [128, 8, 16], F32)
                c.dma("sp", wi.t[:, :, :], wi_d.rearrange("(j p) h -> p j h", p=128), wi.b, writes=[wi.b])
                Eb = [sb(nc, es, "E%d" % i, [128, 512], BF16) for i in range(2)]
                Pm = [sb(nc, es, "Pm%d" % i, [128, 512], BF16) for i in range(2)]
                rz = [sb(nc, es, "rz%d" % i, [128, 512], F32) for i in range(2)]
                ctr = dict(s=0, f=0, q=0)

                def att_step(KT_ap, Q_ap, V_ap, masks, kvs, O, Z, first, last, nh):
                    N = nh * 128
                    i = ctr["s"] % 2
                    ctr["s"] += 1
                    S, E, P = PS[i], Eb[i], Pm[i]
                    c.op("pe", lambda e: e.matmul(S.t[:, 0:N], lhsT=KT_ap[0], rhs=Q_ap[0], start=True, stop=True),
                         reads=[KT_ap[1], Q_ap[1]], writes=[S.b])
                    c.op("act", lambda e: e.activation(out=E.t[:, 0:N], in_=S.t[:, 0:N], func=AF.Exp, scale=SC),
                         reads=[S.b], writes=[E.b])
                    for h in range(nh):
                        sl = slice(h * 128, (h + 1) * 128)
                        c.op("dve", lambda e: e.scalar_tensor_tensor(out=P.t[:, sl], in0=E.t[:, sl], scalar=kvs[0], in1=masks[0],
                                                                     op0=ALU.mult, op1=ALU.mult),
                             reads=[E.b, masks[1]] + ([kvs[1]] if kvs[1] is not None else []), writes=[P.b])
                    c.op("pe", lambda e: e.matmul(O.t[:, 0:N], lhsT=V_ap[0], rhs=P.t[:, 0:N], start=first, stop=last),
                         reads=[V_ap[1], P.b], writes=[O.b])
                    c.op("pe", lambda e: e.matmul(Z.t[:, 0:N], lhsT=ones_b, rhs=P.t[:, 0:N], start=first, stop=last),
                         reads=[cstb.b, P.b], writes=[Z.b])

                def finalize(O, Z, oT, h0, nh, n, sink_h0=None):
                    N = nh * 128
                    r = rz[ctr["f"] % 2]
                    ctr["f"] += 1
                    if sink_h0 is not None:
                        for h in range(nh):
                            sl = slice(h * 128, (h + 1) * 128)
                            c.op("dve", lambda e: e.tensor_scalar(out=r.t[:, sl], in0=Z.t[:, sl],
                                                                  scalar1=esk.t[:, sink_h0 + h:sink_h0 + h + 1], scalar2=None,
                                                                  op0=ALU.add), reads=[Z.b, esk.b], writes=[r.b])
                        c.op("dve", lambda e: e.reciprocal(out=r.t[:, 0:N], in_=r.t[:, 0:N]), reads=[r.b], writes=[r.b])
                    else:
                        c.op("dve", lambda e: e.reciprocal(out=r.t[:, 0:N], in_=Z.t[:, 0:N]), reads=[Z.b], writes=[r.b])
                    for h in range(nh):
                        sl = slice(h * 128, (h + 1) * 128)
                        c.op("dve", lambda e: e.tensor_tensor(out=oT.t[:, h0 + h, n * 128:(n + 1) * 128], in0=O.t[:, sl],
                                                              in1=r.t[:, sl], op=ALU.mult), reads=[O.b, r.b], writes=[oT.b])

                with contextlib.ExitStack() as e2:
                    QA = sb(nc, e2, "QA", [128, 10, NT], BF16)
                    c.dma("sp", QA.t[:, :, :], fm_d[F_QA:F_QA + 10].rearrange("h p t -> p h t"), QA.b, writes=[QA.b])
                    KA = sb(nc, e2, "KA", [128, 2, 1152], BF16)
                    c.dma("sp", KA.t[:, :, :], kwa_d.rearrange("h p t -> p h t"), KA.b, writes=[KA.b])
                    VA = sb(nc, e2, "VA", [128, 9, 256], BF16)
                    c.dma("sp", VA.t[:, :, :], vwa_d.rearrange("(b p) d -> p b d", p=128), VA.b, writes=[VA.b])
                    for n in range(8):
                        for gi, (kvh, h0, nh) in enumerate(((0, 0, 4), (0, 4, 1), (1, 5, 4), (1, 9, 1))):
                            O, Z = PS[2 + (gi % 2) * 2], PS[3 + (gi % 2) * 2]
                            for st in range(2):
                                kb = n + st
                                att_step((KA.t[:, kvh, kb * 128:(kb + 1) * 128], KA.b),
                                         (QA.t[:, h0:h0 + nh, n * 128:(n + 1) * 128], QA.b),
                                         (VA.t[:, kb, kvh * 128:(kvh + 1) * 128], VA.b),
                                         (mk.t[:, 1 if st == 0 else 0, :], mk.b), (kvd.t[:, kb:kb + 1], kvd.b),
                                         O, Z, st == 0, st == 1, nh)
                            finalize(O, Z, oTa, h0, nh, n, sink_h0=h0)
                    c.barrier()
                with contextlib.ExitStack() as e2:
                    QC = sb(nc, e2, "QC", [128, 12, NT], BF16)
                    c.dma("sp", QC.t[:, :, :], fm_d[F_QC:F_QC + 12].rearrange("h p t -> p h t"), QC.b, writes=[QC.b])
                    KCw = sb(nc, e2, "KCw", [128, NWC * 128], BF16)
                    c.dma("sp", KCw.t[:, :], kwc_d, KCw.b, writes=[KCw.b])
                    VC = sb(nc, e2, "VC", [128, NWC, 128], BF16)
                    c.dma("sp", VC.t[:, :, :], vwc_d.rearrange("(b p) d -> p b d", p=128), VC.b, writes=[VC.b])
                    for n in range(8):
                        O, Z = PS[2 + (n % 2) * 2], PS[3 + (n % 2) * 2]
                        for g in range(3):
                            WB = WB_C[g]
                            for dl in range(WB + 1):
                                rb = WOFF_C[g] + n + WB - dl
                                mi = (0 if dl == 0 else 2) if g == 0 else (3 * g + (0 if dl == 0 else (2 if dl == WB else 1)))
                                att_step((KCw.t[:, rb * 128:(rb + 1) * 128], KCw.b),
                                         (QC.t[:, g * 4:(g + 1) * 4, n * 128:(n + 1) * 128], QC.b),
                                         (VC.t[:, rb, :], VC.b), (mk.t[:, mi, :], mk.b), (kvd.t[:, 9 + rb:9 + rb + 1], kvd.b),
                                         O, Z, (g == 0 and dl == 0), (g == 2 and dl == WB), 4)
                        finalize(O, Z, oTc, 0, 4, n)
                    c.barrier()
                with contextlib.ExitStack() as e2:
                    KIlo = sb(nc, e2, "KIlo", [128, 4096], BF16)
                    KIhi = sb(nc, e2, "KIhi", [128, 4096], BF16)
                    c.op("pool", lambda e: e.memset(KIlo.t[:, :], 0.0), writes=[KIlo.b])
                    c.op("pool", lambda e: e.memset(KIhi.t[:, :], 0.0), writes=[KIhi.b])
                    c.dma("sp", KIlo.t[0:64, :], ki_d[0:64, :], KIlo.b, writes=[KIlo.b])
                    c.dma("sp", KIhi.t[64:128, :], ki_d[64:128, :], KIhi.b, writes=[KIhi.b])
                    KB = sb(nc, e2, "KB", [128, 4096], BF16)
                    c.dma("sp", KB.t[:, :], kb_d, KB.b, writes=[KB.b])
                    VB = sb(nc, e2, "VB", [128, 32, 128], BF16)
                    c.dma("sp", VB.t[:, :, :], vb_d.rearrange("(b p) d -> p b d", p=128), VB.b, writes=[VB.b])
                    acc = sb(nc, e2, "acc", [128, 4096], F32)
                    work = sb(nc, e2, "work", [128, 4096], F32)
                    mT = sb(nc, e2, "mT", [128, 32, 128], BF16)
                    m8 = sb(nc, e2, "m8", [128, 8], F32)
                    tk = sb(nc, e2, "tk", [128, 1], F32)
                    Rl = [sb(nc, e2, "Rl%d" % i, [128, 512], F32) for i in range(3)]
                    QI = [sb(nc, e2, "QI%d" % i, [128, 8, 128], BF16) for i in range(2)]
                    QB = [sb(nc, e2, "QB%d" % i, [128, 10, 128], BF16) for i in range(2)]
                    lctr = 0
                    for n in range(8):
                        qi, qb = QI[n % 2], QB[n % 2]
                        c.dma("sp", qi.t[:, :, :], fm_d[F_QI:F_QI + 8, :, n * 128:(n + 1) * 128].rearrange("h p t -> p h t"),
                              qi.b, writes=[qi.b])
                        c.dma("sp", qb.t[:, :, :], fm_d[F_QB:F_QB + 10, :, n * 128:(n + 1) * 128].rearrange("h p t -> p h t"),
                              qb.b, writes=[qb.b])
                        c.dma("sp", acc.t[:, :], iota_d, acc.b, writes=[acc.b])
                        c.op("dve", lambda e: e.tensor_scalar(out=acc.t[:, :], in0=acc.t[:, :], scalar1=thr.t[:, n:n + 1],
                                                              scalar2=NEG, op0=ALU.is_gt, op1=ALU.mult),
                             reads=[acc.b, thr.b], writes=[acc.b])
                        for kc4 in range(8):
                            ks = slice(kc4 * 512, (kc4 + 1) * 512)
                            for h in range(16):
                                L = PS[lctr % 4]
                                R_ = Rl[lctr % 3]
                                lctr += 1
                                kt = KIlo if h % 2 == 0 else KIhi
                                c.op("pe", lambda e: e.matmul(L.t[:, :], lhsT=qi.t[:, h // 2, :], rhs=kt.t[:, ks], start=True, stop=True),
                                     reads=[qi.b, kt.b], writes=[L.b])
                                c.op("act", lambda e: e.activation(out=R_.t[:, :], in_=L.t[:, :], func=AF.Relu), reads=[L.b], writes=[R_.b])
                                c.op("dve", lambda e: e.scalar_tensor_tensor(out=acc.t[:, ks], in0=R_.t[:, :], scalar=wi.t[:, n, h:h + 1],
                                                                             in1=acc.t[:, ks], op0=ALU.mult, op1=ALU.add),
                                     reads=[R_.b, wi.b, acc.b], writes=[acc.b])
                        for r in range(32):
                            src = acc if r == 0 else work
                            c.op("dve", lambda e: e.max(out=m8.t[:, :], in_=src.t[:, :]), reads=[src.b], writes=[m8.b])
                            if r < 31:
                                c.op("dve", lambda e: e.match_replace(out=work.t[:, :], in_to_replace=m8.t[:, :], in_values=src.t[:, :],
                                                                      imm_value=NEG), reads=[src.b, m8.b], writes=[work.b])
                        c.op("dve", lambda e: e.tensor_scalar(out=tk.t[:, :], in0=m8.t[:, 7:8], scalar1=-1.0e29, scalar2=None, op0=ALU.max),
                             reads=[m8.b], writes=[tk.b])
                        c.op("dve", lambda e: e.tensor_scalar(out=work.t[:, :], in0=acc.t[:, :], scalar1=tk.t[:, 0:1], scalar2=None,
                                                              op0=ALU.is_ge), reads=[acc.b, tk.b], writes=[work.b])
                        for g4 in range(8):
                            P = PS[g4 % 2]
                            for i in range(4):
                                kb = g4 * 4 + i
                                c.op("pe", lambda e: e.transpose(out=P.t[:, i * 128:(i + 1) * 128], in_=work.t[:, kb * 128:(kb + 1) * 128],
                                                                 identity=cst.t[:, 0, :]), reads=[work.b, cst.b], writes=[P.b])
                            dst = mT.t[:, g4 * 4:(g4 + 1) * 4, :]
                            srcp = P.t[:, :].rearrange("p (c t) -> p c t", c=4)
                            if g4 % 2 == 0:
                                c.op("act", lambda e: e.copy(out=dst, in_=srcp), reads=[P.b], writes=[mT.b])
                            else:
                                c.op("dve", lambda e: e.tensor_copy(out=dst, in_=srcp), reads=[P.b], writes=[mT.b])
                        groups = ((0, 4, PS[2], PS[3]), (4, 4, PS[4], PS[5]), (8, 2, PS[6], PS[7]))
                        for kb in range(32):
                            for (h0, nh, O, Z) in groups:
                                att_step((KB.t[:, kb * 128:(kb + 1) * 128], KB.b), (qb.t[:, h0:h0 + nh, :], qb.b),
                                         (VB.t[:, kb, :], VB.b), (mT.t[:, kb, :], mT.b), (1.0, None), O, Z, kb == 0, kb == 31, nh)
                        for (h0, nh, O, Z) in groups:
                            finalize(O, Z, oTb, h0, nh, n)
                    c.barrier()
            yT = sb(nc, esO, "yT", [128, KC, NT], BF16)
            with contextlib.ExitStack() as es:
                ws = WStream(c, nc, es, 256, 10, nstg=2, nslots=6, tag="wbr")
                GT = [sb(nc, es, "GT%d" % i, [128, NT], F32) for i in range(6)]
                ya = [sb(nc, es, "ya%d" % i, [128, NT], F32) for i in range(2)]
                tb = [sb(nc, es, "tb%d" % i, [128, NT], F32) for i in range(2)]
                brs = ((wa_d, 1280, oTa, 10), (wb_d, 1280, oTb, 10), (wc_d, 512, oTc, 4))

                def ldblk(bi):
                    for br, (wd_, nr, _, _) in enumerate(brs):
                        ws.load((bi % 2) * 3 + br, wd_, nr, [(bi * 256, 256, 0)])

                ldblk(0)
                pctr = 0
                for bi in range(16):
                    if bi + 1 < 16:
                        ldblk(bi + 1)
                    for ci in range(2):
                        m = bi * 2 + ci
                        y, t2 = ya[m % 2], tb[m % 2]
                        for br, (wd_, nr, oT, nk) in enumerate(brs):
                            g = GT[(m % 2) * 3 + br]
                            c.dma("sp", g.t[:, :], gt_d[br * 32 + m], g.b, writes=[g.b])
                            wt_, wbb = ws.slots[(bi % 2) * 3 + br]
                            for th in range(2):
                                P = PS[pctr % 4]
                                pctr += 1
                                H = slice(th * 512, (th + 1) * 512)
                                for k in range(nk):
                                    c.op("pe", lambda e: e.matmul(P.t[:, :], lhsT=wt_[:, k, ci * 128:(ci + 1) * 128], rhs=oT.t[:, k, H],
                                                                  start=(k == 0), stop=(k == nk - 1)),
                                         reads=[wbb[k // 8], oT.b], writes=[P.b])
                                if br == 0:
                                    c.op("dve", lambda e: e.tensor_tensor(out=y.t[:, H], in0=P.t[:, :], in1=g.t[:, H], op=ALU.mult),
                                         reads=[P.b, g.b], writes=[y.b])
                                else:
                                    c.op("dve", lambda e: e.tensor_tensor(out=t2.t[:, H], in0=P.t[:, :], in1=g.t[:, H], op=ALU.mult),
                                         reads=[P.b, g.b], writes=[t2.b])
                                    if br == 1:
                                        c.op("pool", lambda e: e.tensor_tensor(out=y.t[:, H], in0=y.t[:, H], in1=t2.t[:, H], op=ALU.add),
                                             reads=[y.b, t2.b], writes=[y.b])
                                    else:
                                        c.op("pool", lambda e: e.tensor_tensor(out=yT.t[:, m, H], in0=y.t[:, H], in1=t2.t[:, H], op=ALU.add),
                                             reads=[y.b, t2.b], writes=[yT.b])
                c.barrier()
            with contextlib.ExitStack() as es:
                wo = WStream(c, nc, es, 256, KC, nstg=3, nslots=2, tag="wo")
                R = [sb(nc, es, "R%d" % i, [128, 256], F32) for i in range(2)]
                Oo = [sb(nc, es, "O%d" % i, [128, 256], F32) for i in range(2)]
                wo.load(0, wo_d, D, [(0, 256, 0)])
                ectr = 0
                for cb in range(16):
                    if cb + 1 < 16:
                        wo.load((cb + 1) % 2, wo_d, D, [((cb + 1) * 256, 256, 0)])
                    wt_, wbb = wo.slots[cb % 2]
                    for j in range(8):
                        P = PS[ectr % 4]
                        r, o = R[ectr % 2], Oo[ectr % 2]
                        ectr += 1
                        for k in range(KC):
                            c.op("pe", lambda e: e.matmul(P.t[:, 0:256], lhsT=yT.t[:, k, j * 128:(j + 1) * 128], rhs=wt_[:, k, 0:256],
                                                          start=(k == 0), stop=(k == KC - 1)), reads=[wbb[k // 8], yT.b], writes=[P.b])
                        c.dma("sp", r.t[:, :], x_d[j * 128:(j + 1) * 128, cb * 256:(cb + 1) * 256], r.b, writes=[r.b])
                        c.op("dve", lambda e: e.tensor_tensor(out=o.t[:, :], in0=P.t[:, 0:256], in1=r.t[:, :], op=ALU.add),
                             reads=[P.b, r.b], writes=[o.b])
                        c.dma("sp", xm_d[j * 128:(j + 1) * 128, cb * 256:(cb + 1) * 256], o.t[:, :], out_b, reads=[o.b])
                c.barrier()
    return nc


def rope_np(T, dim):
    pos = np.arange(T, dtype=np.float32)
    inv = (np.float32(10000.0) ** (-np.arange(0, dim, 2, dtype=np.float32) / np.float32(dim))).astype(np.float32)
    ang = (pos[:, None] * inv[None, :]).astype(np.float32)
    return np.cos(ang).astype(np.float32), np.sin(ang).astype(np.float32)


def make_cs(tok0):
    cos, sin = rope_np(4096, 128)
    cos, sin = cos[tok0:tok0 + NT], sin[tok0:tok0 + NT]
    cosi, sini = rope_np(4096, 64)
    cosi, sini = cosi[tok0:tok0 + NT], sini[tok0:tok0 + NT]
    cs = np.zeros((128, 4, NT), np.float32)
    cs[:, 0, :] = np.concatenate([cos, cos], 1).T
    cs[:, 1, :] = np.concatenate([-sin, sin], 1).T
    cs[:, 2, :] = np.concatenate([cosi, cosi, cosi, cosi], 1).T
    cs[:, 3, :] = np.concatenate([-sini, sini, -sini, sini], 1).T
    return cs


def make_cst():
    cst = np.zeros((128, 4, 128), np.float32)
    cst[:, 0, :] = np.eye(128)
    cst[:, 1, :] = 1.0
    i = np.arange(128)
    cst[(i + 64) % 128, 2, i] = 1.0
    cst[(i // 64) * 64 + ((i % 64) + 32) % 64, 3, i] = 1.0
    return cst


_NC = {}


def get_nc(name, fn):
    if name not in _NC:
        _NC[name] = fn()
    return _NC[name]


def run_L1(x_cores, w_in_l, g1_l, gain_l):
    nc = get_nc("L1", build_L1)
    cst = make_cst()
    g1r = np.ascontiguousarray(np.broadcast_to(g1_l[None, :], (128, D)))
    gn = np.ascontiguousarray(gain_l.T)
    in_maps = []
    for cid in range(8):
        in_maps.append({"x": x_cores[cid], "w_in": w_in_l, "g1": g1r, "gain": gn,
                        "cs": make_cs((cid % 4) * NT), "cst": cst})
    res = run_bass_kernel_spmd(nc, in_maps, core_ids=list(range(8)))
    return res.results


def make_cw(conv_w_l, conv_b_l):
    a = np.concatenate([conv_w_l.T, conv_b_l[:, None]], axis=1)
    return np.ascontiguousarray(a.reshape(172, 128, 4).transpose(1, 0, 2))


def run_L3(xm_cores, halo_cores, g2_l, wup_l, wdn_l, cw):
    nc = get_nc("L3", build_L3)
    cst = make_cst()
    g2r = np.ascontiguousarray(np.broadcast_to(g2_l[None, :], (128, D)))
    in_maps = []
    for cid in range(8):
        in_maps.append({"xm": xm_cores[cid], "halo": halo_cores[cid], "g2": g2r, "w_up": wup_l, "w_down": wdn_l,
                        "cw": cw, "cst": cst})
    res = run_bass_kernel_spmd(nc, in_maps, core_ids=list(range(8)))
    return res.results


def make_masks():
    kl = np.arange(128)[:, None]
    ql = np.arange(128)[None, :]
    d = ql - kl
    m = np.zeros((128, 9, 128), np.float32)
    m[:, 0] = d >= 0
    m[:, 1] = d < 0
    m[:, 2] = d <= 0
    for gi, dil in ((1, 4), (2, 16)):
        res = (d % dil) == 0
        m[:, 3 * gi + 0] = res & (d >= 0)
        m[:, 3 * gi + 1] = res
        m[:, 3 * gi + 2] = res & (d <= 0)
    return m


def make_iota():
    return np.ascontiguousarray((np.arange(4096)[None, :] - np.arange(128)[:, None]).astype(np.float32))


def make_kvalid(q):
    kv = np.zeros((128, 9 + NWC), np.float32)
    for kb in range(9):
        kv[:, kb] = 1.0 if (8 * q - 1 + kb) >= 0 else 0.0
    for g in range(3):
        for rb in range(WB_C[g] + 8):
            kv[:, 9 + WOFF_C[g] + rb] = 1.0 if (8 * q - WB_C[g] + rb) >= 0 else 0.0
    return kv


def window(a, t0, t1, axis):
    pad = max(0, -t0)
    sl = [slice(None)] * a.ndim
    sl[axis] = slice(max(t0, 0), t1)
    part = a[tuple(sl)]
    if pad:
        shp = list(part.shape)
        shp[axis] = pad
        part = np.concatenate([np.zeros(shp, a.dtype), part], axis=axis)
    return np.ascontiguousarray(part)


def run_L2(r1, x_cores, sinks_l, wa, wb, wc, wo):
    nc = get_nc("L2", build_L2)
    cst, masks, iota = make_cst(), make_masks(), make_iota()
    skr = np.ascontiguousarray(np.broadcast_to(sinks_l[None, :], (128, 10)))
    in_maps = []
    for b in range(2):
        fms = [np.asarray(r1[b * 4 + q]["fm"]) for q in range(4)]
        vs = [np.asarray(r1[b * 4 + q]["v"]) for q in range(4)]
        kseq = np.concatenate([f[[F_KA, F_KA + 1, F_KB, F_KI, F_KCC, F_KCC + 1, F_KCC + 2]] for f in fms], axis=2)
        vseq = np.concatenate(vs, axis=0)
        for q in range(4):
            cid = b * 4 + q
            t1 = (q + 1) * NT
            kwc = np.concatenate([window(kseq[4 + g], q * NT - WB_C[g] * 128, t1, 1) for g in range(3)], axis=1)
            vwc = np.concatenate([window(vseq[:, 384 + g * 128:384 + (g + 1) * 128], q * NT - WB_C[g] * 128, t1, 0)
                                  for g in range(3)], axis=0)
            in_maps.append({
                "fm": fms[q], "kwa": window(kseq[0:2], q * NT - 128, t1, 2), "vwa": window(vseq[:, 0:256], q * NT - 128, t1, 0),
                "kwc": kwc, "vwc": vwc, "kb": np.ascontiguousarray(kseq[2]), "ki": np.ascontiguousarray(kseq[3]),
                "vb": np.ascontiguousarray(vseq[:, 256:384]), "wi": np.asarray(r1[cid]["wi"]), "gt": np.asarray(r1[cid]["gt"]),
                "x": x_cores[cid], "sinks": skr, "kvalid": make_kvalid(q),
                "thr": np.ascontiguousarray(np.broadcast_to(((8 * q + np.arange(8)) * 128).astype(np.float32)[None, :], (128, 8))),
                "iota": iota, "masks": masks, "cst": cst, "w_br_a": wa, "w_br_b": wb, "w_br_c": wc, "w_out": wo})
    res = run_bass_kernel_spmd(nc, in_maps, core_ids=list(range(8)))
    return res.results


def kernel(x, norm1_g, w_in, qk_gain, sinks, w_br_a, w_br_b, w_br_c, w_out, norm2_g, w_up, conv_w, conv_b, w_down):
    f = lambda a: np.ascontiguousarray(np.asarray(a, dtype=np.float32))
    x = f(x)
    xc = [np.ascontiguousarray(x[cid // 4, (cid % 4) * NT:(cid % 4 + 1) * NT]) for cid in range(8)]
    for l in range(2):
        r1 = run_L1(xc, f(w_in[l]), f(norm1_g[l]), f(qk_gain[l]))
        r2 = run_L2(r1, xc, f(sinks[l]), f(w_br_a[l]), f(w_br_b[l]), f(w_br_c[l]), f(w_out[l]))
        xm = [np.asarray(r2[cid]["xm"]) for cid in range(8)]
        halo = [np.zeros((2, D), np.float32) if cid % 4 == 0 else np.ascontiguousarray(xm[cid - 1][NT - 2:NT]) for cid in range(8)]
        r3 = run_L3(xm, halo, f(norm2_g[l]), f(w_up[l]), f(w_down[l]), make_cw(f(conv_w[l]), f(conv_b[l])))
        xc = [np.asarray(r3[cid]["xo"]) for cid in range(8)]
    out = np.zeros((2, 4096, D), np.float32)
    for cid in range(8):
        out[cid // 4, (cid % 4) * NT:(cid % 4 + 1) * NT] = xc[cid]
    return out
```
